# Optimizing a Trainium2 kernel written in Bass

```python
import math
import jax, jax.numpy as jnp
from jax import lax
import numpy as np

D_MODEL = 1024
BATCH = 1
SEQ = 16384
DEPTH = 4

HEAD_DIM = 64
N_MIXERS = 4
MIX_WIDTH = D_MODEL
N_HEADS = MIX_WIDTH // HEAD_DIM
HEADS_PER_MIXER = N_HEADS // N_MIXERS
GROUP_WIDTH = HEADS_PER_MIXER * HEAD_DIM
D_FF = 4 * D_MODEL
PLE_DIM = 256
QUERY_BLOCK = 128
NSA_CMP_LEN = 32
NSA_CMP_STRIDE = 16
NSA_CMP_HIDDEN = 256
NSA_SEL_LEN = 64
NSA_N_SEL = 16
NSA_WINDOW = 512
NSA_KV = HEAD_DIM
DILATED_PATTERNS = ((128, 1), (512, 4), (2048, 16))
N_ALIBI_HEADS = 2 * HEADS_PER_MIXER
FORGET_BIAS_INIT = 3.0
RMS_EPS = 1e-6
SEL_FORCE = 1e9
SPLITS = (GROUP_WIDTH, NSA_KV, NSA_KV, NSA_KV, NSA_KV, NSA_KV, NSA_KV, 3 * HEADS_PER_MIXER,
          GROUP_WIDTH, GROUP_WIDTH, GROUP_WIDTH,
          GROUP_WIDTH, GROUP_WIDTH, GROUP_WIDTH, HEADS_PER_MIXER,
          GROUP_WIDTH, GROUP_WIDTH, GROUP_WIDTH)
N_IN = sum(SPLITS)
F32 = jnp.float32

kernel_name = 'hybrid_parallel_heads_nsa_dilated_fox_stickbreak'


def rmsnorm(x, g):
    xf = x.astype(F32)
    y = xf * lax.rsqrt(jnp.mean(xf * xf, axis=-1, keepdims=True) + RMS_EPS)
    return (y * g.astype(F32)).astype(x.dtype)


def alibi_slopes():
    return (2.0 ** (-8.0 * np.arange(1, N_ALIBI_HEADS + 1) / N_ALIBI_HEADS)).astype(np.float32)


def masked_softmax(s, mask):
    m = jnp.max(jnp.where(mask, s, -jnp.inf), axis=-1, keepdims=True)
    m = jnp.where(jnp.isfinite(m), m, 0.0)
    e = jnp.where(mask, jnp.exp(s - m), 0.0)
    den = jnp.sum(e, axis=-1, keepdims=True)
    return e / jnp.maximum(den, 1e-30)


def to_blocks(t):
    B, H, S = t.shape[:3]
    t = t.reshape((B, H, S // QUERY_BLOCK, QUERY_BLOCK) + t.shape[3:])
    return jnp.moveaxis(t, 2, 0)


def from_blocks(t):
    nq, B, H, Q, Dh = t.shape
    return jnp.moveaxis(t, 0, 2).reshape(B, H, nq * Q, Dh)


def block_positions(S):
    return jnp.arange(S, dtype=jnp.int32).reshape(S // QUERY_BLOCK, QUERY_BLOCK)


def banded_attention(q, k, v, back, dist_scale, slopes):
    B, G, R, L, Dh = q.shape
    bq = math.gcd(L, QUERY_BLOCK)
    nb = L // bq
    span = bq + back
    idx = np.arange(nb)[:, None] * bq + np.arange(span)[None, :]
    pad = ((0, 0), (0, 0), (back, 0), (0, 0))
    kb = jnp.pad(k, pad)[:, :, idx]
    vb = jnp.pad(v, pad)[:, :, idx]
    qb = q.reshape(B, G, R, nb, bq, Dh)
    s = jnp.einsum('bgrnqd,bgnkd->bgrnqk', qb, kb, preferred_element_type=F32) * (Dh ** -0.5)
    dist = np.arange(bq)[:, None] + back - np.arange(span)[None, :]
    valid = (dist >= 0) & (dist <= back) & ((idx - back)[:, None, :] >= 0)
    s = s - (slopes * dist_scale)[:, :, None, None, None] * dist.astype(np.float32)
    s = jnp.where(valid, s, -jnp.inf)
    lse = jax.nn.logsumexp(s, axis=-1)
    pr = jnp.exp(s - lse[..., None])
    out = jnp.einsum('bgrnqk,bgnkd->bgrnqd', pr.astype(v.dtype), vb, preferred_element_type=F32)
    return out.reshape(B, G, R, L, Dh).astype(q.dtype), lse.reshape(B, G, R, L)


def nsa_mixer(q, k_cmp, v_cmp, k_sel, v_sel, k_win, v_win, gate_logits, b_gate,
              pos_k, w1_k, w2_k, pos_v, w1_v, w2_v, slopes):
    B, H, S, Dh = q.shape
    scale = Dh ** -0.5
    n_cmp = (S - NSA_CMP_LEN) // NSA_CMP_STRIDE + 1
    cstart = np.arange(n_cmp) * NSA_CMP_STRIDE
    cidx = cstart[:, None] + np.arange(NSA_CMP_LEN)[None, :]
    cend = cstart + NSA_CMP_LEN - 1
    cmid = (cstart + (NSA_CMP_LEN - 1) / 2.0).astype(np.float32)

    def compress(t, pos, w1, w2):
        blocks = t[:, cidx] + pos
        flat = blocks.reshape(B, n_cmp, NSA_CMP_LEN * Dh)
        return jax.nn.gelu(flat @ w1) @ w2

    kc = compress(k_cmp, pos_k, w1_k, w2_k)
    vc = compress(v_cmp, pos_v, w1_v, w2_v)
    n_blk = S // NSA_SEL_LEN
    n_sel = min(NSA_N_SEL, n_blk)
    bstart = np.arange(n_blk) * NSA_SEL_LEN
    cover = ((cstart[:, None] < bstart[None, :] + NSA_SEL_LEN) & (cend[:, None] >= bstart[None, :])).astype(np.float32)
    bidx = jnp.arange(n_blk)
    sl = slopes[:, None, None]

    def block(args):
        qb, tq = args
        tf = tq.astype(F32)
        s = jnp.einsum('bhqd,bcd->bhqc', qb, kc, preferred_element_type=F32) * scale
        s = s - sl * (tf[:, None] - cmid[None, :])
        p_c = masked_softmax(s, cend[None, :] <= tq[:, None])
        o_c = jnp.einsum('bhqc,bcd->bhqd', p_c.astype(vc.dtype), vc, preferred_element_type=F32)
        imp = jnp.einsum('bhqc,cn->bqn', p_c, cover)
        cur = tq // NSA_SEL_LEN
        forced = (bidx[None, :] == 0) | (bidx[None, :] == cur[:, None]) | (bidx[None, :] == cur[:, None] - 1)
        elig = bstart[None, :] <= tq[:, None]
        score = jnp.where(forced, SEL_FORCE, jnp.where(elig, imp, -SEL_FORCE))
        _, sel = lax.top_k(score, n_sel)
        tok = (sel[..., None] * NSA_SEL_LEN + jnp.arange(NSA_SEL_LEN)).reshape(B, QUERY_BLOCK, n_sel * NSA_SEL_LEN)
        bi = jnp.arange(B)[:, None, None]
        kg = k_sel[bi, tok]
        vg = v_sel[bi, tok]
        dist = tq[None, :, None] - tok
        s2 = jnp.einsum('bhqd,bqkd->bhqk', qb, kg, preferred_element_type=F32) * scale
        s2 = s2 - sl * dist[:, None].astype(F32)
        p_s = masked_softmax(s2, (dist >= 0)[:, None])
        o_s = jnp.einsum('bhqk,bqkd->bhqd', p_s.astype(vg.dtype), vg, preferred_element_type=F32)
        return o_c, o_s

    o_c, o_s = lax.map(block, (to_blocks(q), block_positions(S)))
    o_w, _ = banded_attention(q[:, None], k_win[:, None], v_win[:, None], NSA_WINDOW - 1, 1, slopes[None, :])
    g = jax.nn.sigmoid((gate_logits + b_gate).astype(F32)).reshape(B, S, 3, H).transpose(2, 0, 3, 1)[..., None]
    out = g[0] * from_blocks(o_c) + g[1] * from_blocks(o_s) + g[2] * o_w[:, 0].astype(F32)
    return out.astype(q.dtype)


def dilated_mixer(q, k, v, slopes):
    B, H, S, Dh = q.shape
    outs, lses = [], []
    for window, d in DILATED_PATTERNS:
        L = S // d

        def sub(t):
            return t.reshape(B, H, L, d, Dh).transpose(0, 1, 3, 2, 4).reshape(B, H * d, L, Dh)

        sl = np.repeat(slopes, d).reshape(H * d, 1)
        o, lse = banded_attention(sub(q)[:, :, None], sub(k), sub(v), window // d, d, sl)
        outs.append(o.reshape(B, H, d, L, Dh).transpose(0, 1, 3, 2, 4).reshape(B, H, S, Dh).astype(F32))
        lses.append(lse.reshape(B, H, d, L).transpose(0, 1, 3, 2).reshape(B, H, S))
    w = jax.nn.softmax(jnp.stack(lses), axis=0)
    return jnp.sum(w[..., None] * jnp.stack(outs), axis=0).astype(q.dtype)


def forgetting_mixer(q, k, v, f_logits, b_f):
    B, H, S, Dh = q.shape
    scale = Dh ** -0.5
    logf = jax.nn.log_sigmoid((f_logits + b_f).astype(F32)).transpose(0, 2, 1)
    c = jnp.cumsum(logf, axis=-1)
    spos = jnp.arange(S, dtype=jnp.int32)

    def block(args):
        qb, cq, tq = args
        s = jnp.einsum('bhqd,bhkd->bhqk', qb, k, preferred_element_type=F32) * scale
        s = s + (cq[..., None] - c[:, :, None, :])
        s = jnp.where(spos[None, :] <= tq[:, None], s, -jnp.inf)
        pr = jax.nn.softmax(s, axis=-1)
        return jnp.einsum('bhqk,bhkd->bhqd', pr.astype(v.dtype), v, preferred_element_type=F32)

    o = lax.map(block, (to_blocks(q), to_blocks(c), block_positions(S)))
    return from_blocks(o).astype(q.dtype)


def stick_breaking_mixer(q, k, v):
    B, H, S, Dh = q.shape
    scale = Dh ** -0.5
    spos = jnp.arange(S, dtype=jnp.int32)

    def block(args):
        qb, tq = args
        z = jnp.einsum('bhqd,bhkd->bhqk', qb, k, preferred_element_type=F32) * scale
        strict = spos[None, :] < tq[:, None]
        log_keep = jnp.where(strict, jax.nn.log_sigmoid(-z), 0.0)
        later = lax.cumsum(log_keep, axis=3, reverse=True) - log_keep
        a = jnp.where(strict, jnp.exp(jax.nn.log_sigmoid(z) + later), 0.0)
        return jnp.einsum('bhqk,bhkd->bhqd', a.astype(v.dtype), v, preferred_element_type=F32)

    o = lax.map(block, (to_blocks(q), block_positions(S)))
    return from_blocks(o).astype(q.dtype)


def setup_inputs(seed: int = 0) -> dict:
    key = jax.random.key(seed)
    ks = jax.random.split(key, 24)

    def nrm(k, shape, scale):
        return jax.random.normal(k, shape, F32) * scale

    def gain(k, shape):
        return 1.0 + 0.1 * jax.random.normal(k, shape, F32)

    H = HEADS_PER_MIXER
    flat = NSA_CMP_LEN * HEAD_DIM
    return {
        'x': nrm(ks[0], (BATCH, SEQ, D_MODEL), 1.0),
        'p': nrm(ks[1], (DEPTH, BATCH, SEQ, PLE_DIM), 1.0),
        'g_mix': gain(ks[2], (DEPTH, D_MODEL)),
        'w_in': nrm(ks[3], (DEPTH, D_MODEL, N_IN), D_MODEL ** -0.5),
        'b_nsa_gate': nrm(ks[4], (DEPTH, 3 * H), 0.1),
        'b_forget': FORGET_BIAS_INIT + nrm(ks[5], (DEPTH, H), 0.5),
        'cmp_pos_k': nrm(ks[6], (DEPTH, NSA_CMP_LEN, HEAD_DIM), 0.1),
        'cmp_w1_k': nrm(ks[7], (DEPTH, flat, NSA_CMP_HIDDEN), flat ** -0.5),
        'cmp_w2_k': nrm(ks[8], (DEPTH, NSA_CMP_HIDDEN, HEAD_DIM), NSA_CMP_HIDDEN ** -0.5),
        'cmp_pos_v': nrm(ks[9], (DEPTH, NSA_CMP_LEN, HEAD_DIM), 0.1),
        'cmp_w1_v': nrm(ks[10], (DEPTH, flat, NSA_CMP_HIDDEN), flat ** -0.5),
        'cmp_w2_v': nrm(ks[11], (DEPTH, NSA_CMP_HIDDEN, HEAD_DIM), NSA_CMP_HIDDEN ** -0.5),
        'g_head': gain(ks[12], (DEPTH, MIX_WIDTH)),
        'w_out': nrm(ks[13], (DEPTH, MIX_WIDTH, D_MODEL), MIX_WIDTH ** -0.5),
        'g_mlp': gain(ks[14], (DEPTH, D_MODEL)),
        'w_up': nrm(ks[15], (DEPTH, D_MODEL, D_FF), D_MODEL ** -0.5),
        'w_down': nrm(ks[16], (DEPTH, D_FF, D_MODEL), D_FF ** -0.5),
        'g_ple': gain(ks[17], (DEPTH, D_MODEL)),
        'w_ple_gate': nrm(ks[18], (DEPTH, D_MODEL, D_MODEL), D_MODEL ** -0.5),
        'b_ple_gate': nrm(ks[19], (DEPTH, D_MODEL), 0.1),
        'w_ple_proj': nrm(ks[20], (DEPTH, PLE_DIM, D_MODEL), PLE_DIM ** -0.5),
        'g_final': gain(ks[21], (D_MODEL,)),
    }


def reference(x, p, g_mix, w_in, b_nsa_gate, b_forget, cmp_pos_k, cmp_w1_k, cmp_w2_k,
              cmp_pos_v, cmp_w1_v, cmp_w2_v, g_head, w_out, g_mlp, w_up, w_down,
              g_ple, w_ple_gate, b_ple_gate, w_ple_proj, g_final):
    B, S, _ = x.shape
    slopes = alibi_slopes()
    slopes_nsa = slopes[0::2]
    slopes_dil = slopes[1::2]
    offs = np.cumsum(SPLITS)[:-1].tolist()

    def heads(t):
        return t.reshape(B, S, HEADS_PER_MIXER, HEAD_DIM).transpose(0, 2, 1, 3)

    h = x
    for i in range(DEPTH):
        u = rmsnorm(h, g_mix[i])
        (q_a, k_cmp, v_cmp, k_sel, v_sel, k_win, v_win, gate_a,
         q_b, k_b, v_b, q_c, k_c, v_c, f_c, q_d, k_d, v_d) = jnp.split(u @ w_in[i], offs, axis=-1)
        o_a = nsa_mixer(heads(q_a), k_cmp, v_cmp, k_sel, v_sel, k_win, v_win, gate_a, b_nsa_gate[i],
                        cmp_pos_k[i], cmp_w1_k[i], cmp_w2_k[i], cmp_pos_v[i], cmp_w1_v[i], cmp_w2_v[i],
                        slopes_nsa)
        o_b = dilated_mixer(heads(q_b), heads(k_b), heads(v_b), slopes_dil)
        o_c = forgetting_mixer(heads(q_c), heads(k_c), heads(v_c), f_c, b_forget[i])
        o_d = stick_breaking_mixer(heads(q_d), heads(k_d), heads(v_d))
        o = jnp.concatenate([o_a, o_b, o_c, o_d], axis=1).astype(h.dtype).transpose(0, 2, 1, 3)
        o = rmsnorm(o, g_head[i].reshape(N_HEADS, HEAD_DIM)).reshape(B, S, MIX_WIDTH)
        h = h + o @ w_out[i]
        u = rmsnorm(h, g_mlp[i])
        h = h + jnp.square(jax.nn.relu(u @ w_up[i])) @ w_down[i]
        gate = jax.nn.sigmoid(rmsnorm(h, g_ple[i]) @ w_ple_gate[i] + b_ple_gate[i])
        h = h + (p[i] @ w_ple_proj[i]) * gate
    return rmsnorm(h, g_final)
```

```python
import contextlib
import math
import numpy as np
import ml_dtypes
import concourse.bass as bass
import concourse.mybir as mybir
from concourse.bass_utils import run_bass_kernel_spmd

F32 = mybir.dt.float32
BF16 = mybir.dt.bfloat16
AF = mybir.ActivationFunctionType
ALU = mybir.AluOpType
NPBF = ml_dtypes.bfloat16

NCORES = 8
D = 1024
DFF = 4096
PLE = 256
NIN = 2960
EPS = 1e-6
NEG = -30000.0
CC_INC = 1

_Q_COLS = list(range(0, 256)) + list(range(652, 908)) + list(range(1420, 1676)) + list(range(2192, 2448))
_KT_COLS = (list(range(256, 320)) + list(range(320, 384)) + list(range(384, 448)) + list(range(512, 576))
            + list(range(908, 1164)) + list(range(1676, 1932)) + list(range(2448, 2704)))
_V_COLS = (list(range(448, 512)) + list(range(576, 640)) + list(range(1164, 1420)) + list(range(1932, 2188))
           + list(range(2704, 2960)))
_GF_COLS = list(range(640, 652)) + list(range(2188, 2192))
PERM = np.array(_Q_COLS + _KT_COLS + _V_COLS + _GF_COLS)
assert len(PERM) == NIN and len(set(PERM.tolist())) == NIN


class Buf:
    __slots__ = ("name", "t", "w", "r", "dsem", "dcount")

    def __init__(self, name, t):
        self.name = name
        self.t = t
        self.w = {}
        self.r = {}
        self.dsem = None
        self.dcount = 0

    def __getitem__(self, idx):
        return self.t[idx]


class KB:
    def __init__(self, nc, stack):
        self.nc = nc
        self.stack = stack
        self.engs = {"pe": nc.tensor, "dve": nc.vector, "act": nc.scalar,
                     "pool": nc.gpsimd, "sp": nc.sync}
        self.sem, self.cnt, self.waited = {}, {}, {}
        for e in self.engs:
            self.sem[e] = stack.enter_context(nc.semaphore("s_" + e))
            self.cnt[e] = 0
            self.waited[e] = {}
        self.ninst = 0
        self.nwait = 0
        self.nsem = len(self.engs)
        self.dbufs = []
        self.semstack = stack
        self.bscratch = stack.enter_context(nc.sbuf_tensor("bscratch", [128, 2], F32))

    pfx = ""

    def sb(self, name, shape, dt=F32):
        name = self.pfx + name
        t = self.stack.enter_context(self.nc.sbuf_tensor(name, list(shape), dt))
        return Buf(name, t)

    def ps(self, name, shape, dt=F32):
        name = self.pfx + name
        t = self.stack.enter_context(self.nc.psum_tensor(name, list(shape), dt))
        return Buf(name, t)

    def dram(self, name, shape, dt=F32, kind="Internal"):
        t = self.nc.dram_tensor(name, list(shape), dt, kind=kind)
        return Buf(name, t.ap())

    def _waits(self, E, reads, writes, skip_sem=None):
        deps = {}
        for b in reads:
            for k, (s, v) in b.w.items():
                if k not in deps or deps[k][1] < v:
                    deps[k] = (s, v)
        for b in writes:
            for d in (b.w, b.r):
                for k, (s, v) in d.items():
                    if k not in deps or deps[k][1] < v:
                        deps[k] = (s, v)
        eng = self.engs[E]
        wd = self.waited[E]
        for k, (s, v) in deps.items():
            if k == E and E == "pe":
                continue
            if skip_sem is not None and k == skip_sem:
                continue
            if wd.get(k, 0) >= v:
                continue
            eng.wait_ge(s, v)
            wd[k] = v
            self.nwait += 1

    def op(self, E, fn, reads=(), writes=()):
        self._waits(E, reads, writes)
        inst = fn(self.engs[E])
        self.cnt[E] += 1
        inst.then_inc(self.sem[E], 1)
        d = (self.sem[E], self.cnt[E])
        for b in reads:
            b.r[E] = d
        for b in writes:
            b.w = {E: d}
            b.r = {}
        self.ninst += 1
        return inst

    def dma(self, Q, out_ap, in_ap, reads=(), writes=(), **kw):
        wb = writes[0]
        if wb.dsem is None:
            wb.dsem = self.semstack.enter_context(self.nc.semaphore("d%d_%s" % (self.nsem, wb.name)))
            self.nsem += 1
            self.dbufs.append(wb)
        key = "d_" + wb.name
        self._waits(Q, reads, writes, skip_sem=key)
        inst = self.engs[Q].dma_start(out=out_ap, in_=in_ap, **kw)
        wb.dcount += 16
        inst.then_inc(wb.dsem, 16)
        d = (wb.dsem, wb.dcount)
        for b in reads:
            b.r[key] = d
        for b in writes:
            b.w = {key: d}
            b.r = {}
        self.ninst += 1
        return inst

    def allgather(self, src, dst):
        if dst.dsem is None:
            dst.dsem = self.semstack.enter_context(self.nc.semaphore("d%d_%s" % (self.nsem, dst.name)))
            self.nsem += 1
            self.dbufs.append(dst)
        key = "d_" + dst.name
        self._waits("pool", [src], [dst], skip_sem=key)
        inst = self.nc.gpsimd.collective_compute("AllGather", ALU.bypass, replica_groups=[list(range(NCORES))],
                                                 ins=[src.t.opt()], outs=[dst.t.opt()])
        dst.dcount += CC_INC
        inst.then_inc(dst.dsem, CC_INC)
        d = (dst.dsem, dst.dcount)
        src.r[key] = d
        dst.w = {key: d}
        dst.r = {}
        self.ninst += 1
        return inst

    def finish(self, bufs, E="sp"):
        self._waits(E, bufs, [])

    def barrier(self):
        pool = self.engs["pool"]
        wd = self.waited["pool"]
        for b in self.dbufs:
            key = "d_" + b.name
            if wd.get(key, 0) < b.dcount:
                pool.wait_ge(b.dsem, b.dcount)
                wd[key] = b.dcount
                self.nwait += 1
        for F in self.engs:
            if F != "pool" and wd.get(F, 0) < self.cnt[F]:
                pool.wait_ge(self.sem[F], self.cnt[F])
                wd[F] = self.cnt[F]
                self.nwait += 1
        inst = pool.memset(self.bscratch[:], 0.0)
        self.cnt["pool"] += 1
        inst.then_inc(self.sem["pool"], 1)
        self.ninst += 1
        for E in self.engs:
            if E != "pool":
                self.engs[E].wait_ge(self.sem["pool"], self.cnt["pool"])
                self.waited[E]["pool"] = self.cnt["pool"]
                self.nwait += 1

    @contextlib.contextmanager
    def scope(self):
        outer = self.stack
        with contextlib.ExitStack() as inner:
            self.stack = inner
            try:
                yield
            finally:
                self.barrier()
                self.stack = outer


@contextlib.contextmanager
def _kbctx(nc, kb, pfx):
    if kb is not None:
        old = kb.pfx
        kb.pfx = pfx
        with kb.scope():
            yield kb
        kb.pfx = old
    else:
        with contextlib.ExitStack() as st:
            yield KB(nc, st)


class Rot:
    def __init__(self, bufs):
        self.bufs = bufs
        self.i = 0

    def next(self):
        b = self.bufs[self.i % len(self.bufs)]
        self.i += 1
        return b


def rms_rstd(kb, src_ap, src_bufs, junk, ss, rstd, epsb, n, col):
    kb.op("pool", lambda e: e.memset(ss[:, col:col + 1], 0.0), [], [ss])
    kb.op("act", lambda e: e.activation(out=junk[:, 0:n], in_=src_ap, func=AF.Square,
                                        accum_out=ss[:, col:col + 1]), list(src_bufs) + [ss], [junk, ss])
    kb.op("act", lambda e: e.activation(out=rstd[:, col:col + 1], in_=ss[:, col:col + 1], func=AF.Ln,
                                        scale=1.0 / n, bias=epsb[:]), [ss, epsb], [rstd])
    kb.op("act", lambda e: e.activation(out=rstd[:, col:col + 1], in_=rstd[:, col:col + 1], func=AF.Exp,
                                        scale=-0.5), [rstd], [rstd])


def load_weight_bf16(kb, wdst, wsrc_ap, wsrc_buf, gcol, stage_rot, kchunks, ncols, qrot, colstep=512):
    engs = ["dve", "pool"]
    n = 0
    for k in range(kchunks):
        for c0 in range(0, ncols, colstep):
            cw = min(colstep, ncols - c0)
            stg = stage_rot.next()
            kb.dma(qrot.next(), stg[:, 0:cw], wsrc_ap[k * 128:(k + 1) * 128, c0:c0 + cw], reads=[wsrc_buf], writes=[stg])
            E = engs[n % 2]
            n += 1
            if gcol is not None:
                kb.op(E, lambda e: e.tensor_scalar(out=wdst[:, k, c0:c0 + cw], in0=stg[:, 0:cw],
                                                   scalar1=gcol[:, k:k + 1], scalar2=None, op0=ALU.mult),
                      [stg, gcol], [wdst])
            else:
                kb.op(E, lambda e: e.tensor_copy(out=wdst[:, k, c0:c0 + cw], in_=stg[:, 0:cw]), [stg], [wdst])


class QRot:
    def __init__(self, qs=("sp", "pool")):
        self.qs = qs
        self.i = 0

    def next(self):
        q = self.qs[self.i % len(self.qs)]
        self.i += 1
        return q


def build_rowlocal(G, do_post, do_pre, do_final, nc=None, kb_in=None, given=None):
    TL = G * 128
    CH = min(4, G)
    NCHUNK = G // CH
    CT = CH * 128
    if nc is None:
        nc = bass.Bass("TRN2", target_bir_lowering=False)
    given = given or {}

    def din(name, shape, dt=F32):
        if name in given:
            return given[name]
        return Buf(name, nc.dram_tensor(name, list(shape), dt, kind="ExternalInput").ap())

    def dout(name, shape, dt=F32):
        if name in given:
            return given[name]
        return Buf(name, nc.dram_tensor(name, list(shape), dt, kind="ExternalOutput").ap())

    h_in = din("h", [TL, D])
    identb_d = din("identb", [128, 128], BF16)
    outs = []
    if do_post:
        o_in = din("o", [TL, D])
        p_in = din("p", [TL, PLE])
        w_out_d = din("w_out", [D, D])
        gcols_d = din("gcols", [128, 24])
        w_up_d = din("w_up", [D, DFF])
        w_down_d = din("w_down", [DFF, D])
        w_pg_d = din("w_pg", [D, D])
        b_pg_d = din("b_pg", [1, D])
        w_pp_d = din("w_pp", [PLE, D])
    if do_pre:
        w_in_d = din("w_in", [D, NIN])
        g_mix_d = din("g_mix", [128, 8])
        b_gf_d = din("b_gf", [1, 16])
        qt_o = dout("qt_o", [1024, TL], BF16)
        kt_o = dout("kt_o", [1024, TL], BF16)
        v_o = dout("v_o", [TL, 896], BF16)
        gf_o = dout("gf_o", [TL, 16])
        outs += [qt_o, kt_o, v_o, gf_o]
    if do_final:
        g_fin_d = din("g_fin", [1, D])
        out_o = dout("out", [TL, D])
        outs += [out_o]
    if do_post and not do_final:
        h_o = dout("h_out", [TL, D])
        outs += [h_o]

    with _kbctx(nc, kb_in, "r_") as kb:
        qrot = QRot()
        identb = kb.sb("identb_s", [128, 128], BF16)
        kb.dma("sp", identb[:], identb_d[:], reads=[identb_d], writes=[identb])
        epsb = kb.sb("epsb", [128, 1])
        kb.op("pool", lambda e: e.memset(epsb[:], EPS), [], [epsb])
        stage = Rot([kb.sb("wstg%d" % i, [128, 512]) for i in range(3)])
        hs = kb.sb("hs", [128, G, D])
        for i in range(G):
            kb.dma(qrot.next(), hs[:, i, :], h_in[i * 128:(i + 1) * 128, :], reads=[h_in], writes=[hs])
        junk = kb.sb("junk", [128, D])
        ss = kb.sb("ss", [128, 4 * G])
        rstd = kb.sb("rstd", [128, 4 * G])
        ub = Rot([kb.sb("ub%d" % i, [128, D], BF16) for i in range(2)])
        uT = kb.sb("uT", [128, 8, TL], BF16)
        ptr = Rot([kb.ps("ptr%d" % i, [128, 1024], BF16) for i in range(2)])
        pmm = Rot([kb.ps("pmm%d" % i, [128, 512]) for i in range(4)])

        def transpose_to_uT(u, i, nk=8, dst=None):
            pt = ptr.next()
            for k in range(nk):
                kb.op("pe", lambda e: e.transpose(pt[:, k * 128:(k + 1) * 128], u[:, k * 128:(k + 1) * 128], identb[:]),
                      [u, identb], [pt])
            if dst is None:
                kb.op("act", lambda e: e.activation(out=uT[:, :, i * 128:(i + 1) * 128],
                                                    in_=pt[:].rearrange("p (k t) -> p k t", k=8), func=AF.Copy),
                      [pt], [uT])
            else:
                kb.op("act", lambda e: e.activation(out=dst[:], in_=pt[:, 0:nk * 128].rearrange("p (k t) -> p k t", k=nk),
                                                    func=AF.Copy), [pt], [dst])

        def norm_transpose(i, col):
            rms_rstd(kb, hs[:, i, :], [hs], junk, ss, rstd, epsb, D, col)
            u = ub.next()
            kb.op("dve", lambda e: e.tensor_scalar(out=u[:], in0=hs[:, i, :], scalar1=rstd[:, col:col + 1],
                                                   scalar2=None, op0=ALU.mult), [hs, rstd], [u])
            transpose_to_uT(u, i)

        def add_to_h(i, hsl, pm):
            kb.op("dve", lambda e: e.tensor_tensor(out=hs[:, i, hsl], in0=hs[:, i, hsl], in1=pm[:], op=ALU.add),
                  [hs, pm], [hs])

        if do_post:
            gcols = kb.sb("gcols_s", [128, 24])
            kb.dma("sp", gcols[:], gcols_d[:], reads=[gcols_d], writes=[gcols])
            with kb.scope():
                wsm = kb.sb("wsm", [128, 8, D], BF16)
                load_weight_bf16(kb, wsm, w_out_d.t, w_out_d, Buf_view(gcols, 0, 8), stage, 8, D, qrot)
                os_rot = Rot([kb.sb("os%d" % i, [128, D]) for i in range(2)])
                for i in range(G):
                    osb = os_rot.next()
                    kb.dma(qrot.next(), osb[:], o_in[i * 128:(i + 1) * 128, :], reads=[o_in], writes=[osb])
                    u = ub.next()
                    kb.op("pool", lambda e: e.tensor_copy(out=u[:], in_=osb[:]), [osb], [u])
                    transpose_to_uT(u, i)
                    for half in range(2):
                        pm = pmm.next()
                        for k in range(8):
                            kb.op("pe", lambda e: e.matmul(pm[:], lhsT=uT[:, k, i * 128:(i + 1) * 128],
                                                           rhs=wsm[:, k, half * 512:(half + 1) * 512],
                                                           start=(k == 0), stop=(k == 7)), [uT, wsm], [pm])
                        add_to_h(i, slice(half * 512, (half + 1) * 512), pm)
            with kb.scope():
                for i in range(G):
                    norm_transpose(i, i)
                wup = kb.sb("wup", [128, 8, 1024], BF16)
                wdn = kb.sb("wdn", [128, 8, D], BF16)
                hid = kb.sb("hid", [128, 8, CT], BF16)
                rl = Rot([kb.sb("rl%d" % i, [128, CT]) for i in range(2)])
                for qf in range(4):
                    load_weight_bf16(kb, wup, w_up_d.t[:, qf * 1024:(qf + 1) * 1024], w_up_d, Buf_view(gcols, 8, 16), stage, 8, 1024, qrot)
                    load_weight_bf16(kb, wdn, w_down_d.t[qf * 1024:(qf + 1) * 1024, :], w_down_d, None, stage, 8, D, qrot)
                    for cki in range(NCHUNK):
                        t0 = cki * CT
                        for nt in range(8):
                            pm = pmm.next()
                            for k in range(8):
                                kb.op("pe", lambda e: e.matmul(pm[:, 0:CT], lhsT=wup[:, k, nt * 128:(nt + 1) * 128],
                                                               rhs=uT[:, k, t0:t0 + CT], start=(k == 0), stop=(k == 7)),
                                      [wup, uT], [pm])
                            r = rl.next()
                            kb.op("act", lambda e: e.activation(out=r[:], in_=pm[:, 0:CT], func=AF.Relu), [pm], [r])
                            E = "dve" if nt % 2 == 0 else "pool"
                            kb.op(E, lambda e: e.tensor_tensor(out=hid[:, nt, :], in0=r[:], in1=r[:], op=ALU.mult), [r], [hid])
                        for bi in range(CH):
                            i = cki * CH + bi
                            for half in range(2):
                                pm = pmm.next()
                                for nt in range(8):
                                    kb.op("pe", lambda e: e.matmul(pm[:], lhsT=hid[:, nt, bi * 128:(bi + 1) * 128],
                                                                   rhs=wdn[:, nt, half * 512:(half + 1) * 512],
                                                                   start=(nt == 0), stop=(nt == 7)), [hid, wdn], [pm])
                                add_to_h(i, slice(half * 512, (half + 1) * 512), pm)
            with kb.scope():
                wsm = kb.sb("wsm2", [128, 8, D], BF16)
                load_weight_bf16(kb, wsm, w_pg_d.t, w_pg_d, Buf_view(gcols, 16, 24), stage, 8, D, qrot)
                wpp = kb.sb("wpp", [128, 2, D], BF16)
                load_weight_bf16(kb, wpp, w_pp_d.t, w_pp_d, None, stage, 2, D, qrot)
                bpg = kb.sb("bpg", [128, D])
                kb.dma("sp", bpg[:], b_pg_d.t.partition_broadcast(128), reads=[b_pg_d], writes=[bpg])
                pT = kb.sb("pT", [128, 2, 128], BF16)
                gt = kb.sb("gt", [128, D])
                ps_rot = Rot([kb.sb("psb%d" % i, [128, PLE]) for i in range(2)])
                for i in range(G):
                    norm_transpose(i, G + i)
                    psb = ps_rot.next()
                    kb.dma(qrot.next(), psb[:], p_in[i * 128:(i + 1) * 128, :], reads=[p_in], writes=[psb])
                    u = ub.next()
                    kb.op("pool", lambda e: e.tensor_copy(out=u[:, 0:PLE], in_=psb[:]), [psb], [u])
                    transpose_to_uT(u, i, nk=2, dst=pT)
                    for half in range(2):
                        hsl = slice(half * 512, (half + 1) * 512)
                        pm = pmm.next()
                        for k in range(8):
                            kb.op("pe", lambda e: e.matmul(pm[:], lhsT=uT[:, k, i * 128:(i + 1) * 128], rhs=wsm[:, k, hsl],
                                                           start=(k == 0), stop=(k == 7)), [uT, wsm], [pm])
                        kb.op("dve", lambda e: e.tensor_tensor(out=gt[:, hsl], in0=pm[:], in1=bpg[:, hsl], op=ALU.add), [pm, bpg], [gt])
                        kb.op("act", lambda e: e.activation(out=gt[:, hsl], in_=gt[:, hsl], func=AF.Exp, scale=-1.0), [gt], [gt])
                        kb.op("pool", lambda e: e.tensor_scalar(out=gt[:, hsl], in0=gt[:, hsl], scalar1=1.0, scalar2=None, op0=ALU.add), [gt], [gt])
                        kb.op("dve", lambda e: e.reciprocal(out=gt[:, hsl], in_=gt[:, hsl]), [gt], [gt])
                        pm2 = pmm.next()
                        for k in range(2):
                            kb.op("pe", lambda e: e.matmul(pm2[:], lhsT=pT[:, k, :], rhs=wpp[:, k, hsl],
                                                           start=(k == 0), stop=(k == 1)), [pT, wpp], [pm2])
                        kb.op("dve", lambda e: e.tensor_tensor(out=gt[:, hsl], in0=gt[:, hsl], in1=pm2[:], op=ALU.mult), [gt, pm2], [gt])
                        kb.op("pool", lambda e: e.tensor_tensor(out=hs[:, i, hsl], in0=hs[:, i, hsl], in1=gt[:, hsl], op=ALU.add), [hs, gt], [hs])
                if not do_final:
                    for i in range(G):
                        kb.dma(qrot.next(), h_o[i * 128:(i + 1) * 128, :], hs[:, i, :], reads=[hs], writes=[h_o])

        if do_final:
            with kb.scope():
                gfin = kb.sb("gfin", [128, D])
                kb.dma("sp", gfin[:], g_fin_d.t.partition_broadcast(128), reads=[g_fin_d], writes=[gfin])
                fo = Rot([kb.sb("fo%d" % i, [128, D]) for i in range(2)])
                for i in range(G):
                    rms_rstd(kb, hs[:, i, :], [hs], junk, ss, rstd, epsb, D, 2 * G + i)
                    f = fo.next()
                    kb.op("dve", lambda e: e.scalar_tensor_tensor(out=f[:], in0=hs[:, i, :], scalar=rstd[:, 2 * G + i:2 * G + i + 1],
                                                                  in1=gfin[:], op0=ALU.mult, op1=ALU.mult), [hs, rstd, gfin], [f])
                    kb.dma(qrot.next(), out_o[i * 128:(i + 1) * 128, :], f[:], reads=[f], writes=[out_o])

        if do_pre:
            with kb.scope():
                gmix = kb.sb("gmix", [128, 8])
                kb.dma("sp", gmix[:], g_mix_d[:], reads=[g_mix_d], writes=[gmix])
                bgf = kb.sb("bgf", [128, 16])
                kb.dma("sp", bgf[:], b_gf_d.t.partition_broadcast(128), reads=[b_gf_d], writes=[bgf])
                win = kb.sb("win", [128, 8, NIN], BF16)
                load_weight_bf16(kb, win, w_in_d.t, w_in_d, gmix, stage, 8, NIN, qrot)
                for i in range(G):
                    norm_transpose(i, 3 * G + i)
                fst = Rot([kb.sb("fst%d" % i, [128, 512], BF16) for i in range(3)])
                vst = Rot([kb.sb("vst%d" % i, [128, 896], BF16) for i in range(2)])
                gst = Rot([kb.sb("gst%d" % i, [128, 16]) for i in range(2)])
                for cki in range(NCHUNK):
                    t0 = cki * CT
                    for mt in range(16):
                        pm = pmm.next()
                        for k in range(8):
                            kb.op("pe", lambda e: e.matmul(pm[:, 0:CT], lhsT=win[:, k, mt * 128:(mt + 1) * 128],
                                                           rhs=uT[:, k, t0:t0 + CT], start=(k == 0), stop=(k == 7)), [win, uT], [pm])
                        f = fst.next()
                        sc = 0.125 if mt < 8 else 1.0
                        if mt % 2 == 0:
                            kb.op("act", lambda e: e.activation(out=f[:, 0:CT], in_=pm[:, 0:CT], func=AF.Copy, scale=sc), [pm], [f])
                        else:
                            kb.op("dve", lambda e: e.tensor_scalar(out=f[:, 0:CT], in0=pm[:, 0:CT], scalar1=sc, scalar2=None, op0=ALU.mult), [pm], [f])
                        dst = qt_o if mt < 8 else kt_o
                        r0 = (mt % 8) * 128
                        kb.dma(qrot.next(), dst[r0:r0 + 128, t0:t0 + CT], f[:, 0:CT], reads=[f], writes=[dst])
                    for bi in range(CH):
                        i = cki * CH + bi
                        vs = vst.next()
                        for (c0, cw) in ((0, 512), (512, 384)):
                            pm = pmm.next()
                            for k in range(8):
                                kb.op("pe", lambda e: e.matmul(pm[:, 0:cw], lhsT=uT[:, k, i * 128:(i + 1) * 128],
                                                               rhs=win[:, k, 2048 + c0:2048 + c0 + cw], start=(k == 0), stop=(k == 7)),
                                      [uT, win], [pm])
                            kb.op("act", lambda e: e.activation(out=vs[:, c0:c0 + cw], in_=pm[:, 0:cw], func=AF.Copy), [pm], [vs])
                        kb.dma(qrot.next(), v_o[i * 128:(i + 1) * 128, :], vs[:], reads=[vs], writes=[v_o])
                        pm = pmm.next()
                        for k in range(8):
                            kb.op("pe", lambda e: e.matmul(pm[:, 0:16], lhsT=uT[:, k, i * 128:(i + 1) * 128],
                                                           rhs=win[:, k, 2944:2960], start=(k == 0), stop=(k == 7)), [uT, win], [pm])
                        g = gst.next()
                        kb.op("dve", lambda e: e.tensor_tensor(out=g[:], in0=pm[:, 0:16], in1=bgf[:], op=ALU.add), [pm, bgf], [g])
                        kb.op("act", lambda e: e.activation(out=g[:], in_=g[:], func=AF.Exp, scale=-1.0), [g], [g])
                        kb.op("dve", lambda e: e.tensor_scalar(out=g[:], in0=g[:], scalar1=1.0, scalar2=None, op0=ALU.add), [g], [g])
                        kb.op("act", lambda e: e.activation(out=g[:, 12:16], in_=g[:, 12:16], func=AF.Ln), [g], [g])
                        kb.op("dve", lambda e: e.tensor_scalar(out=g[:, 12:16], in0=g[:, 12:16], scalar1=-1.0, scalar2=None, op0=ALU.mult), [g], [g])
                        kb.op("dve", lambda e: e.reciprocal(out=g[:, 0:12], in_=g[:, 0:12]), [g], [g])
                        kb.dma(qrot.next(), gf_o[i * 128:(i + 1) * 128, :], g[:], reads=[g], writes=[gf_o])
        if kb_in is None:
            kb.finish(outs)
            print("rowlocal program: ninst=%d nwait=%d nsem=%d" % (kb.ninst, kb.nwait, kb.nsem))
    return nc, outs


def Buf_view(buf, c0, c1):
    v = Buf(buf.name, buf.t[:, c0:c1])
    v.w = buf.w
    v.r = buf.r
    return v


SL_NSA = [2.0 ** -1, 2.0 ** -3, 2.0 ** -5, 2.0 ** -7]
SL_DIL = [2.0 ** -2, 2.0 ** -4, 2.0 ** -6, 2.0 ** -8]


def attn_consts(G, c):
    S = 1024 * G
    NT = 8 * G
    TL = 128 * G
    NB = S // 64
    NCMP = S // 16 - 1
    NJ = (NCMP + 127) // 128
    ps = np.arange(128)[:, None].astype(np.int64)
    pt = np.arange(128)[None, :].astype(np.int64)
    out = {}
    cm = np.zeros((8, 128, 128), np.float32)
    sm = np.zeros((8, 128, 128), np.float32)
    for r in range(8):
        if r < c:
            cm[r] = 0.0
            sm[r] = 1.0
        elif r == c:
            cm[r] = np.where(ps <= pt, 0.0, NEG)
            sm[r] = (ps < pt).astype(np.float32)
        else:
            cm[r] = NEG
            sm[r] = 0.0
    out["cm"] = np.ascontiguousarray(cm.transpose(1, 0, 2))
    out["sm"] = np.ascontiguousarray(sm.transpose(1, 0, 2))
    dtab = np.zeros((24, 128, 128), np.float32)
    for k in range(24):
        d = (c + 16 - k) * 128 + pt - ps
        mult = ((d >= 0) & (d <= 128)).astype(np.int64) + ((d >= 0) & (d <= 512) & (d % 4 == 0)) \
            + ((d >= 0) & (d <= 2048) & (d % 16 == 0))
        dtab[k] = np.where(mult > 0, np.log(np.maximum(mult, 1)), NEG)
    out["dtab"] = np.ascontiguousarray(dtab.transpose(1, 0, 2)).astype(np.float32)
    wtab = np.zeros((12, 128, 128), np.float32)
    for k in range(12):
        d = (c + 4 - k) * 128 + pt - ps
        wtab[k] = np.where((d >= 0) & (d <= 511), 0.0, NEG)
    out["wtab"] = np.ascontiguousarray(wtab.transpose(1, 0, 2))
    ctab = np.zeros((3, 128, 128), np.float32)
    for dl in range(3):
        ctab[dl] = np.where(16 * ps + 31 <= 128 * (8 * dl + c) + pt, 0.0, NEG)
    out["ctab"] = np.ascontiguousarray(ctab.transpose(1, 0, 2))
    tl = ((8 * np.arange(G)[:, None] + c) * 128 + np.arange(128)[None, :]).reshape(-1)
    thi, tlo = tl // 128, tl % 128

    def qrows(slopes):
        q = np.zeros((4, 4, TL), np.float32)
        for h, sl in enumerate(slopes):
            q[0, h] = -sl * 128 * thi
            q[1, h] = -sl * tlo
            q[2, h] = sl * 128
            q[3, h] = sl
        return q.astype(NPBF)
    out["qp_nsa"] = qrows(SL_NSA)
    out["qp_dil"] = qrows(SL_DIL)
    s = np.arange(S)
    kp = np.stack([np.ones(S), np.ones(S), s // 128, s % 128]).astype(np.float32)
    out["kp"] = kp.astype(NPBF)
    cmid = 16 * np.arange(NJ * 128) + 15.5
    kpc = np.stack([np.ones(NJ * 128), np.ones(NJ * 128), np.floor(cmid / 128), cmid - 128 * np.floor(cmid / 128)]).astype(np.float32)
    out["kpc"] = kpc.astype(NPBF)
    NJJ = min(NT, 64)
    e = np.zeros((128, NJJ, 128), np.float32)
    for jj in range(NJJ):
        e[2 * jj, jj, 0:64] = 1.0
        e[2 * jj + 1, jj, 64:128] = 1.0
    out["etab"] = e.astype(NPBF)
    cov = np.zeros((NJ * 128, NB + 1), np.float32)
    j = np.arange(NCMP)
    cstart = j * 16
    cend = cstart + 31
    bstart = np.arange(NB) * 64
    cov[:NCMP, :NB] = ((cstart[:, None] < bstart[None, :] + 64) & (cend[:, None] >= bstart[None, :])).astype(np.float32)
    cov[:NCMP, NB] = 1.0
    out["cov"] = np.ascontiguousarray(cov.reshape(NJ, 128, NB + 1).transpose(1, 0, 2)).astype(NPBF)
    bidx = np.arange(NB)[None, :]
    cur = (tl // 64)[:, None]
    forced = (bidx == 0) | (bidx == cur) | (bidx == cur - 1)
    elig = bstart[None, :] <= tl[:, None]
    out["elig"] = np.where(forced, 0.0, np.where(elig, 1.0, 0.0)).astype(np.float32)
    out["fbias"] = np.where(forced, 1e9, np.where(elig, 0.0, -1e9)).astype(np.float32)
    out["tri_incl"] = (np.arange(128)[:, None] <= np.arange(128)[None, :]).astype(np.float32)
    out["strict"] = (np.arange(128)[:, None] < np.arange(128)[None, :]).astype(np.float32)
    out["onesf"] = np.ones((128, 128), np.float32)
    out["identf"] = np.eye(128, dtype=np.float32)
    out["identb"] = np.eye(128).astype(NPBF)
    sel = np.zeros((128, G), np.float32)
    for i in range(G):
        sel[8 * i + c, i] = 1.0
    out["selm"] = sel[:NT] if NT <= 128 else sel
    out["trineg"] = (-(np.arange(128)[:, None] >= np.arange(128)[None, :]).astype(np.float32)).astype(NPBF)
    out["onesneg"] = (-np.ones((128, 128), np.float32)).astype(NPBF)
    return out


CONST_SPECS = None


def build_attn(G, parts=("nsa", "dil", "fgt", "sb"), nc=None, kb_in=None, given=None):
    S = 1024 * G
    NT = 8 * G
    TL = 128 * G
    NB = S // 64
    NBP = min(NB, 128)
    NH = max(1, NB // 128)
    NCMP = S // 16 - 1
    NJ = (NCMP + 127) // 128
    NJJ = min(NT, 64)
    CH = min(4, G)
    NCHUNK = G // CH
    CT = CH * 128
    if nc is None:
        nc = bass.Bass("TRN2", target_bir_lowering=False)
    given = given or {}

    def din(name, shape, dt=F32):
        if name in given:
            return given[name]
        return Buf(name, nc.dram_tensor(name, list(shape), dt, kind="ExternalInput").ap())

    qt_d = din("qt", [1024, TL], BF16)
    kt_d = din("kt", [1024, S], BF16)
    v_d = din("v", [S, 896], BF16)
    gfl_d = din("gfl", [TL, 16])
    logf_d = din("logf", [128, 4, NT])
    cm_d = din("cm", [128, 8, 128]); sm_d = din("sm", [128, 8, 128])
    dtab_d = din("dtab", [128, 24, 128]); wtab_d = din("wtab", [128, 12, 128]); ctab_d = din("ctab", [128, 3, 128])
    qpn_d = din("qp_nsa", [4, 4, TL], BF16); qpd_d = din("qp_dil", [4, 4, TL], BF16)
    kp_d = din("kp", [4, S], BF16); kpc_d = din("kpc", [4, NJ * 128], BF16)
    etab_d = din("etab", [128, NJJ, 128], BF16)
    cov_d = din("cov", [128, NJ, NB + 1], BF16)
    elig_d = din("elig", [TL, NB]); fb_d = din("fbias", [TL, NB])
    tri_d = din("tri_incl", [128, 128]); strict_d = din("strict", [128, 128]); onesf_d = din("onesf", [128, 128])
    identf_d = din("identf", [128, 128]); identb_d = din("identb", [128, 128], BF16)
    selm_d = din("selm", [NT, G])
    trineg_d = din("trineg", [128, 128], BF16); onesneg_d = din("onesneg", [128, 128], BF16)
    w1_d = din("w1", [2, 2048, 256]); w2_d = din("w2", [2, 256, 64]); posT_d = din("posT", [128, 32])
    o_d = given["o"] if "o" in given else Buf("o", nc.dram_tensor("o", [TL, D], F32, kind="ExternalOutput").ap())
    csn_d = Buf("csn", nc.dram_tensor("csn", [4, 3, S], BF16, kind="Internal").ap())
    csq_d = Buf("csq", nc.dram_tensor("csq", [4, 3, TL], BF16, kind="Internal").ap())

    with _kbctx(nc, kb_in, "a_") as kb:
        qrot = QRot()

        def load(name, src, shape, dt=F32, q="sp"):
            b = kb.sb(name, shape, dt)
            kb.dma(q, b[:], src[:], reads=[src], writes=[b])
            return b

        identf = load("identf_s", identf_d, [128, 128])
        identb = load("identb_s", identb_d, [128, 128], BF16)
        cm = load("cm_s", cm_d, [128, 8, 128], q="pool")
        epsb = kb.sb("epsb", [128, 1])
        kb.op("pool", lambda e: e.memset(epsb[:], EPS), [], [epsb])
        oneb = kb.sb("oneb", [128, 1])
        kb.op("pool", lambda e: e.memset(oneb[:], 1.0), [], [oneb])
        gsb = kb.sb("gsb", [128, G, 16])
        kb.dma("sp", gsb[:], gfl_d.t.rearrange("(i p) c -> p i c", p=128), reads=[gfl_d], writes=[gsb])

        sps = Rot([kb.ps("sps%d" % i, [128, 512]) for i in range(3)])
        accr = Rot([kb.ps("acc%d" % i, [128, 512]) for i in range(2)])
        pfin = kb.ps("pfin", [128, 512])
        pimp = kb.ps("pimp", [128, 512])
        ptr = Rot([kb.sb("pt%d" % i, [128, 512], BF16) for i in range(5)])
        fsr = Rot([kb.sb("fs%d" % i, [65, 512]) for i in range(2)])
        Kaug = kb.sb("Kaug", [72, S], BF16)
        Vaug = kb.sb("Vaug", [128, NT, 65], BF16)
        kb.op("pool", lambda e: e.memset(Vaug[:, :, 64:65], 1.0), [], [Vaug])
        rd = kb.sb("rd", [128, 8])
        ohead = Rot([kb.sb("ohead%d" % i, [128, CH, 64]) for i in range(2)])
        hn_junk = kb.sb("hn_junk", [128, 64])
        hn_ss = kb.sb("hn_ss", [128, CH])
        hn_r = kb.sb("hn_r", [128, CH])
        ost = Rot([kb.sb("ost%d" % i, [128, CH, 64]) for i in range(2)])

        def load_K(row0, aug_src=None, aug_rows=None):
            kb.dma("sp", Kaug[0:64, :], kt_d[row0:row0 + 64, :], reads=[kt_d], writes=[Kaug])
            if aug_src is not None:
                kb.dma("pool", Kaug[aug_rows[0]:aug_rows[1], :], aug_src[0], reads=[aug_src[1]], writes=[Kaug])

        def load_V(col0):
            nsp = 4 if NT >= 64 else 1
            jt = NT // nsp
            for q in range(nsp):
                kb.dma(qrot.next(), Vaug[:, q * jt:(q + 1) * jt, 0:64],
                       v_d.t[q * jt * 128:(q + 1) * jt * 128, col0:col0 + 64].rearrange("(j p) d -> p j d", p=128),
                       reads=[v_d], writes=[Vaug])

        PIPE_D = 2

        def mk_step(kd, j, q_ap, qbufs, n, ncols3=None, mask=None, extra=None, acc_ap=None, accb=None,
                    first=False, after=None):
            st_ = {}

            def s1():
                sp_ = sps.next()
                so = sp_[:, 0:n] if ncols3 is None else sp_[:, 0:n].rearrange("p (a b) -> p a b", a=ncols3)
                kb.op("pe", lambda e: e.matmul(so, lhsT=Kaug[0:kd, j * 128:(j + 1) * 128], rhs=q_ap, start=True,
                                               stop=(extra is None)), [Kaug] + qbufs, [sp_])
                if extra is not None:
                    kb.op("pe", lambda e: e.matmul(sp_[:, 0:n], lhsT=extra[0], rhs=extra[1], start=False, stop=True),
                          extra[2], [sp_])
                if mask is not None:
                    m_ap, m_bufs, mc0, mn, mrep = mask
                    if mrep is None:
                        kb.op("dve", lambda e: e.tensor_tensor(out=sp_[:, mc0:mc0 + mn], in0=sp_[:, mc0:mc0 + mn], in1=m_ap, op=ALU.add),
                              [sp_] + m_bufs, [sp_])
                    else:
                        v3 = sp_[:, 0:n].rearrange("p (a b) -> p a b", a=mrep)
                        kb.op("dve", lambda e: e.tensor_tensor(out=v3, in0=v3, in1=m_ap.unsqueeze(1).to_broadcast([128, mrep, 128]), op=ALU.add),
                              [sp_] + m_bufs, [sp_])
                ptb = ptr.next()
                kb.op("act", lambda e: e.activation(out=ptb[:, 0:n], in_=sp_[:, 0:n], func=AF.Exp), [sp_], [ptb])
                st_["p"] = ptb

            def s2():
                ptb = st_["p"]
                kb.op("pe", lambda e: e.matmul(acc_ap, lhsT=Vaug[:, j, :], rhs=ptb[:, 0:n], start=first, stop=False,
                                               skip_group_check=True), [Vaug, ptb], [accb])
                if after is not None:
                    after()
            return s1, s2

        def run_pipe(steps, D=PIPE_D):
            n_ = len(steps)
            for t in range(n_ + D):
                if t < n_:
                    steps[t][0]()
                if t - D >= 0:
                    steps[t - D][1]()

        def finalize(accb, nunits, consume):
            fs = fsr.next()
            n = nunits * 128
            kb.op("act", lambda e: e.activation(out=fs[:, 0:n], in_=accb[0:65, 0:n], func=AF.Copy), [accb], [fs])
            for u in range(nunits):
                kb.op("pe", lambda e: e.transpose(pfin[:, u * 65:(u + 1) * 65], fs[:, u * 128:(u + 1) * 128], identf[0:65, 0:65]),
                      [fs, identf], [pfin])
            for u in range(nunits):
                kb.op("dve", lambda e: e.tensor_scalar(out=rd[:, u:u + 1], in0=pfin[:, u * 65 + 64:u * 65 + 65], scalar1=1e-30,
                                                       scalar2=None, op0=ALU.max), [pfin], [rd])
            kb.op("dve", lambda e: e.reciprocal(out=rd[:, 0:nunits], in_=rd[:, 0:nunits]), [rd], [rd])
            for u in range(nunits):
                consume(u, pfin[:, u * 65:u * 65 + 64], rd[:, u:u + 1])

        def headnorm_store(src, srcbuf, nb, i0, hg):
            for b in range(nb):
                kb.op("pool", lambda e: e.memset(hn_ss[:, b:b + 1], 0.0), [], [hn_ss])
                kb.op("act", lambda e: e.activation(out=hn_junk[:], in_=src[:, b, :], func=AF.Square,
                                                    accum_out=hn_ss[:, b:b + 1]), [srcbuf, hn_ss], [hn_junk, hn_ss])
            kb.op("act", lambda e: e.activation(out=hn_r[:, 0:nb], in_=hn_ss[:, 0:nb], func=AF.Ln, scale=1.0 / 64, bias=epsb[:]),
                  [hn_ss, epsb], [hn_r])
            kb.op("act", lambda e: e.activation(out=hn_r[:, 0:nb], in_=hn_r[:, 0:nb], func=AF.Exp, scale=-0.5), [hn_r], [hn_r])
            o_s = ost.next()
            kb.op("dve", lambda e: e.tensor_tensor(out=o_s[:, 0:nb, :], in0=src, in1=hn_r[:, 0:nb].unsqueeze(2).to_broadcast([128, nb, 64]),
                                                   op=ALU.mult), [srcbuf, hn_r], [o_s])
            kb.dma(qrot.next(), o_d.t[i0 * 128:(i0 + nb) * 128, hg * 64:(hg + 1) * 64].rearrange("(b p) d -> p b d", p=128),
                   o_s[:, 0:nb, :], reads=[o_s], writes=[o_d])

        def causal_head(kd, q_of, qbufs, extra_of, consume_of):
            steps = []
            for ck in range(NCHUNK):
                i0 = ck * CH
                accb = accr.next()
                nj = 8 * (i0 + CH - 1) + 8
                for j in range(nj):
                    imin = max(i0, j // 8)
                    c0 = (imin - i0) * 128
                    n = CT - c0
                    mask = None
                    if j // 8 >= i0:
                        mask = (cm[:, j % 8, :], [cm], 0, 128, None)
                    extra = extra_of(j, i0 * 128 + c0, n) if extra_of is not None else None
                    after = None
                    if j == nj - 1:
                        after = (lambda accb=accb, i0=i0: finalize(accb, CH, consume_of(i0)))
                    steps.append(mk_step(kd, j, q_of(i0 * 128 + c0, n), qbufs, n, mask=mask, extra=extra,
                                         acc_ap=accb[0:65, c0:CT], accb=accb, first=(j == 0), after=after))
            run_pipe(steps)

        if "nsa" in parts:
            with kb.scope():
                Qn = kb.sb("Qn", [72, 4, TL], BF16)
                for h in range(4):
                    kb.dma(qrot.next(), Qn[0:64, h, :], qt_d[h * 64:(h + 1) * 64, :], reads=[qt_d], writes=[Qn])
                kb.dma("sp", Qn[64:68, :, :], qpn_d[:], reads=[qpn_d], writes=[Qn])
                oacc = kb.sb("oacc", [128, G, 4, 64])
                MnegT = kb.sb("MnegT", [128, NH, TL], BF16)
                Kc = kb.sb("Kc", [72, NJ * 128], BF16)
                Vc = kb.sb("Vc", [128, NJ, 65], BF16)
                with kb.scope():
                    W1 = kb.sb("W1", [64, 32, 256], BF16)
                    w1st = Rot([kb.sb("w1st%d" % i, [64, 8, 256]) for i in range(2)])
                    w2st = kb.sb("w2st", [128, 2, 2, 64])
                    for kv in range(2):
                        kb.dma("sp", w2st[:, kv, :, :], w2_d.t[kv].rearrange("(a p) n -> p a n", p=128), reads=[w2_d], writes=[w2st])
                    W2 = kb.sb("W2", [128, 2, 2, 64], BF16)
                    kb.op("dve", lambda e: e.tensor_copy(out=W2[:], in_=w2st[:]), [w2st], [W2])
                    posf = load("posf", posT_d, [128, 32])
                    posb = kb.sb("posb", [64, 2, 32], BF16)
                    posf2 = kb.sb("posf2", [64, 2, 32])
                    kb.dma("sp", posf2[:, 0, :], posT_d[0:64, :], reads=[posT_d], writes=[posf2])
                    kb.dma("sp", posf2[:, 1, :], posT_d[64:128, :], reads=[posT_d], writes=[posf2])
                    kb.op("dve", lambda e: e.tensor_copy(out=posb[:], in_=posf2[:]), [posf2], [posb])
                    hbias = kb.sb("hbias", [128, 4])
                    hid = kb.sb("hidc", [128, 2, 2, NJ * 128], BF16)
                    kb.op("pool", lambda e: e.memset(hid[:], 0.0), [], [hid])
                    gx = kb.sb("gx", [128, 512]); gy = kb.sb("gy", [128, 512]); gz = kb.sb("gz", [128, 512])
                    for kv in range(2):
                        kb.dma("sp", Kaug[0:64, :], kt_d[kv * 64:(kv + 1) * 64, :], reads=[kt_d], writes=[Kaug])
                        for lg in range(4):
                            stg = w1st.next()
                            kb.dma(qrot.next(), stg[:], w1_d.t[kv, lg * 512:(lg + 1) * 512, :].rearrange("(l d) n -> d l n", d=64),
                                   reads=[w1_d], writes=[stg])
                            kb.op("dve" if lg % 2 == 0 else "pool",
                                  lambda e: e.tensor_copy(out=W1[:, lg * 8:(lg + 1) * 8, :], in_=stg[:]), [stg], [W1])
                        for half in range(2):
                            col = kv * 2 + half
                            for l in range(32):
                                kb.op("pe", lambda e: e.matmul(pimp[:, col:col + 1], lhsT=W1[:, l, half * 128:(half + 1) * 128],
                                                               rhs=posb[:, kv, l:l + 1], start=(l == 0), stop=(l == 31)), [W1, posb], [pimp])
                            kb.op("dve", lambda e: e.tensor_copy(out=hbias[:, col:col + 1], in_=pimp[:, col:col + 1]), [pimp], [hbias])
                        for half in range(2):
                            col = kv * 2 + half
                            for cc0 in range(0, NCMP, 512):
                                cn = min(512, NCMP - cc0)
                                sp_ = sps.next()
                                for l in range(32):
                                    a = 16 * cc0 + l
                                    kb.op("pe", lambda e: e.matmul(sp_[:, 0:cn], lhsT=W1[:, l, half * 128:(half + 1) * 128],
                                                                   rhs=Kaug[0:64, a:a + 16 * (cn - 1) + 1:16], start=(l == 0), stop=(l == 31)),
                                          [W1, Kaug], [sp_])
                                kb.op("act", lambda e: e.activation(out=gx[:, 0:cn], in_=sp_[:, 0:cn], func=AF.Identity,
                                                                    bias=hbias[:, col:col + 1]), [sp_, hbias], [gx])
                                kb.op("pool", lambda e: e.tensor_tensor(out=gy[:, 0:cn], in0=gx[:, 0:cn], in1=gx[:, 0:cn], op=ALU.mult), [gx], [gy])
                                kb.op("dve", lambda e: e.tensor_scalar(out=gy[:, 0:cn], in0=gy[:, 0:cn], scalar1=0.044715, scalar2=1.0,
                                                                       op0=ALU.mult, op1=ALU.add), [gy], [gy])
                                kb.op("pool", lambda e: e.tensor_tensor(out=gy[:, 0:cn], in0=gy[:, 0:cn], in1=gx[:, 0:cn], op=ALU.mult), [gy, gx], [gy])
                                kb.op("act", lambda e: e.activation(out=gz[:, 0:cn], in_=gy[:, 0:cn], func=AF.Tanh, scale=0.7978845608028654), [gy], [gz])
                                kb.op("dve", lambda e: e.tensor_scalar(out=gz[:, 0:cn], in0=gz[:, 0:cn], scalar1=1.0, scalar2=None, op0=ALU.add), [gz], [gz])
                                kb.op("dve", lambda e: e.scalar_tensor_tensor(out=hid[:, kv, half, cc0:cc0 + cn], in0=gx[:, 0:cn], scalar=0.5,
                                                                              in1=gz[:, 0:cn], op0=ALU.mult, op1=ALU.mult), [gx, gz], [hid])
                    for cc0 in range(0, NJ * 128, 512):
                        cn = min(512, NJ * 128 - cc0)
                        sp_ = sps.next()
                        for half in range(2):
                            kb.op("pe", lambda e: e.matmul(sp_[0:64, 0:cn], lhsT=W2[:, 0, half, :], rhs=hid[:, 0, half, cc0:cc0 + cn],
                                                           start=(half == 0), stop=(half == 1)), [W2, hid], [sp_])
                        kb.op("act", lambda e: e.activation(out=Kc[0:64, cc0:cc0 + cn], in_=sp_[0:64, 0:cn], func=AF.Copy), [sp_], [Kc])
                    kb.dma("sp", Kc[64:68, :], kpc_d[:], reads=[kpc_d], writes=[Kc])
                    kb.op("pool", lambda e: e.memset(Vc[:, :, 64:65], 1.0), [], [Vc])
                    for J in range(NJ):
                        sp_ = sps.next()
                        for half in range(2):
                            kb.op("pe", lambda e: e.matmul(sp_[:, 0:64], lhsT=hid[:, 1, half, J * 128:(J + 1) * 128], rhs=W2[:, 1, half, :],
                                                           start=(half == 0), stop=(half == 1)), [hid, W2], [sp_])
                        kb.op("act", lambda e: e.activation(out=Vc[:, J, 0:64], in_=sp_[:, 0:64], func=AF.Copy), [sp_], [Vc])
                with kb.scope():
                    ctab = load("ctab_s", ctab_d, [128, 3, 128])
                    cov = load("cov_s", cov_d, [128, NJ, NB + 1], BF16)
                    PTall = kb.sb("PTall", [128, NJ, 512], BF16)
                    impacc = kb.sb("impacc", [128, NB])
                    eli = Rot([kb.sb("eli%d" % i, [128, NB]) for i in range(2)])
                    fbi = Rot([kb.sb("fbi%d" % i, [128, NB]) for i in range(2)])
                    score = kb.sb("score", [128, NB]); score2 = kb.sb("score2", [128, NB])
                    m8a = kb.sb("m8a", [128, 8]); m8b = kb.sb("m8b", [128, 8])
                    mneg = kb.sb("mneg", [128, NB], BF16)
                    rdi = kb.sb("rdi", [128, 1])
                    ptb16 = kb.ps("ptb16", [128, 1024], BF16)
                    for i in range(G):
                        accb = accr.next()
                        Jmax = min(i // 2, NJ - 1)
                        for J in range(Jmax + 1):
                            mask = None
                            if i - 2 * J <= 2:
                                mask = (ctab[:, i - 2 * J, :], [ctab], 0, 512, 4)
                            sp_ = sps.next()
                            kb.op("pe", lambda e: e.matmul(sp_[:].rearrange("p (a b) -> p a b", a=4), lhsT=Kc[0:68, J * 128:(J + 1) * 128],
                                                           rhs=Qn[0:68, :, i * 128:(i + 1) * 128], start=True, stop=True), [Kc, Qn], [sp_])
                            if mask is not None:
                                v3 = sp_[:].rearrange("p (a b) -> p a b", a=4)
                                kb.op("dve", lambda e: e.tensor_tensor(out=v3, in0=v3, in1=mask[0].unsqueeze(1).to_broadcast([128, 4, 128]),
                                                                       op=ALU.add), [sp_, ctab], [sp_])
                            kb.op("act", lambda e: e.activation(out=PTall[:, J, :], in_=sp_[:], func=AF.Exp), [sp_], [PTall])
                            kb.op("pe", lambda e: e.matmul(accb[0:65, :], lhsT=Vc[:, J, :], rhs=PTall[:, J, :], start=(J == 0), stop=(J == Jmax),
                                                           skip_group_check=True), [Vc, PTall], [accb])
                        for h in range(4):
                            for J in range(Jmax + 1):
                                kb.op("pe", lambda e: e.matmul(pimp[:, 0:NB + 1], lhsT=PTall[:, J, h * 128:(h + 1) * 128], rhs=cov[:, J, :],
                                                               start=(J == 0), stop=(J == Jmax)), [PTall, cov], [pimp])
                            kb.op("dve", lambda e: e.tensor_scalar(out=rdi[:], in0=pimp[:, NB:NB + 1], scalar1=1e-30, scalar2=None, op0=ALU.max),
                                  [pimp], [rdi])
                            kb.op("dve", lambda e: e.reciprocal(out=rdi[:], in_=rdi[:]), [rdi], [rdi])
                            if h == 0:
                                kb.op("dve", lambda e: e.tensor_scalar(out=impacc[:], in0=pimp[:, 0:NB], scalar1=rdi[:, 0:1], scalar2=None,
                                                                       op0=ALU.mult), [pimp, rdi], [impacc])
                            else:
                                kb.op("dve", lambda e: e.scalar_tensor_tensor(out=impacc[:], in0=pimp[:, 0:NB], scalar=rdi[:, 0:1], in1=impacc[:],
                                                                              op0=ALU.mult, op1=ALU.add), [pimp, rdi, impacc], [impacc])
                        el = eli.next(); fb = fbi.next()
                        kb.dma("sp", el[:], elig_d[i * 128:(i + 1) * 128, :], reads=[elig_d], writes=[el])
                        kb.dma("pool", fb[:], fb_d[i * 128:(i + 1) * 128, :], reads=[fb_d], writes=[fb])
                        kb.op("pool", lambda e: e.tensor_tensor(out=score[:], in0=impacc[:], in1=el[:], op=ALU.mult), [impacc, el], [score])
                        kb.op("pool", lambda e: e.tensor_tensor(out=score[:], in0=score[:], in1=fb[:], op=ALU.add), [score, fb], [score])
                        kb.op("dve", lambda e: e.max(out=m8a[:], in_=score[:]), [score], [m8a])
                        kb.op("dve", lambda e: e.match_replace(out=score2[:], in_to_replace=m8a[:], in_values=score[:], imm_value=-3e9),
                              [m8a, score], [score2])
                        kb.op("dve", lambda e: e.max(out=m8b[:], in_=score2[:]), [score2], [m8b])
                        kb.op("dve", lambda e: e.tensor_tensor(out=score2[:], in0=score[:], in1=m8b[:, 7:8].to_broadcast([128, NB]), op=ALU.is_ge),
                              [score, m8b], [score2])
                        kb.op("dve", lambda e: e.tensor_scalar(out=mneg[:], in0=score2[:], scalar1=-1.0, scalar2=-NEG, op0=ALU.add, op1=ALU.mult),
                              [score2], [mneg])
                        for hh in range(NH):
                            kb.op("pe", lambda e: e.transpose(ptb16[0:NBP, hh * 128:(hh + 1) * 128], mneg[:, hh * 128:hh * 128 + NBP], identb[:]),
                                  [mneg, identb], [ptb16])
                        kb.op("act", lambda e: e.activation(out=MnegT[0:NBP, :, i * 128:(i + 1) * 128],
                                                            in_=ptb16[0:NBP, 0:NH * 128].rearrange("p (a b) -> p a b", a=NH), func=AF.Copy),
                              [ptb16], [MnegT])

                        def cons_c(u, num, rden, i=i):
                            kb.op("dve", lambda e: e.tensor_tensor(out=rd[:, 4 + u:5 + u], in0=rden, in1=gsb[:, i, u:u + 1], op=ALU.mult), [rd, gsb], [rd])
                            kb.op("dve", lambda e: e.tensor_scalar(out=oacc[:, i, u, :], in0=num, scalar1=rd[:, 4 + u:5 + u], scalar2=None, op0=ALU.mult),
                                  [pfin, rd], [oacc])
                        finalize(accb, 4, cons_c)
                with kb.scope():
                    etab = load("etab_s", etab_d, [128, NJJ, 128], BF16)
                    load_K(128, (kp_d[:], kp_d), (64, 68))
                    load_V(0)
                    for h in range(4):
                        def consume_of(i0, h=h):
                            def cons(u, num, rden):
                                i = i0 + u
                                kb.op("dve", lambda e: e.tensor_tensor(out=rd[:, 4 + u:5 + u], in0=rden, in1=gsb[:, i, 4 + h:5 + h], op=ALU.mult), [rd, gsb], [rd])
                                kb.op("dve", lambda e: e.scalar_tensor_tensor(out=oacc[:, i, h, :], in0=num, scalar=rd[:, 4 + u:5 + u], in1=oacc[:, i, h, :],
                                                                              op0=ALU.mult, op1=ALU.add), [pfin, rd, oacc], [oacc])
                            return cons
                        causal_head(68, lambda c0, n, h=h: Qn[0:68, h, c0:c0 + n], [Qn],
                                    lambda j, c0, n: (etab[0:NBP, j % 64, :], MnegT[0:NBP, j // 64, c0:c0 + n], [etab, MnegT]),
                                    consume_of)
                with kb.scope():
                    wtab = load("wtab_s", wtab_d, [128, 12, 128])
                    load_K(192, (kp_d[:], kp_d), (64, 68))
                    load_V(64)
                    steps = []
                    for i in range(G):
                        accb = accr.next()
                        js = [j for j in range(8 * i - 4, 8 * i + 8) if j >= 0]

                        def cons_w(u, num, rden, i=i):
                            kb.op("dve", lambda e: e.tensor_tensor(out=rd[:, 4 + u:5 + u], in0=rden, in1=gsb[:, i, 8 + u:9 + u], op=ALU.mult), [rd, gsb], [rd])
                            kb.op("dve", lambda e: e.scalar_tensor_tensor(out=oacc[:, i, u, :], in0=num, scalar=rd[:, 4 + u:5 + u], in1=oacc[:, i, u, :],
                                                                          op0=ALU.mult, op1=ALU.add), [pfin, rd, oacc], [oacc])
                        for j in js:
                            k = j - (8 * i - 4)
                            after = None
                            if j == js[-1]:
                                after = (lambda accb=accb, cons_w=cons_w: finalize(accb, 4, cons_w))
                            steps.append(mk_step(68, j, Qn[0:68, :, i * 128:(i + 1) * 128], [Qn], 512, ncols3=4,
                                                 mask=(wtab[:, k, :], [wtab], 0, 512, 4), acc_ap=accb[0:65, :], accb=accb,
                                                 first=(j == js[0]), after=after))
                    run_pipe(steps)
                    for h in range(4):
                        for ck in range(NCHUNK):
                            headnorm_store(oacc[:, ck * CH:(ck + 1) * CH, h, :], oacc, CH, ck * CH, h)

        Qa = kb.sb("Qa", [72, TL], BF16)

        def plain_consume(i0, oh):
            def cons(u, num, rden):
                kb.op("dve", lambda e: e.tensor_scalar(out=oh[:, u, :], in0=num, scalar1=rden, scalar2=None, op0=ALU.mult), [pfin, rd], [oh])
            return cons

        if "dil" in parts:
            with kb.scope():
                dtab = load("dtab_s", dtab_d, [128, 24, 128])
                for h in range(4):
                    load_K(256 + 64 * h, (kp_d[:], kp_d), (64, 68))
                    load_V(128 + 64 * h)
                    kb.dma("sp", Qa[0:64, :], qt_d[256 + 64 * h:256 + 64 * (h + 1), :], reads=[qt_d], writes=[Qa])
                    kb.dma("sp", Qa[64:68, :], qpd_d[:, h, :], reads=[qpd_d], writes=[Qa])
                    steps = []
                    for ck in range(NCHUNK):
                        accb = accr.next()
                        first = True
                        oh = ohead.next()

                        def fin_d(accb=accb, oh=oh, ck=ck, h=h):
                            finalize(accb, CH, plain_consume(ck * CH, oh))
                            headnorm_store(oh[:, 0:CH, :], oh, CH, ck * CH, 4 + h)
                        for bi in range(CH):
                            i = ck * CH + bi
                            js = list(range(max(0, 8 * i - 16), 8 * i + 8))
                            for j in js:
                                k = j - (8 * i - 16)
                                after = fin_d if (bi == CH - 1 and j == js[-1]) else None
                                steps.append(mk_step(68, j, Qa[0:68, i * 128:(i + 1) * 128], [Qa], 128, mask=(dtab[:, k, :], [dtab], 0, 128, None),
                                                     acc_ap=accb[0:65, bi * 128:(bi + 1) * 128], accb=accb, first=first, after=after))
                                first = False
                    run_pipe(steps)

        if "fgt" in parts:
            with kb.scope():
                tri = load("tri_s", tri_d, [128, 128]); strict = load("strict_s", strict_d, [128, 128]); onesf = load("onesf_s", onesf_d, [128, 128])
                selm = load("selm_s", selm_d, [NT, G])
                lf = load("lf_s", logf_d, [128, 4, NT])
                totT = kb.sb("totT", [128, 128]); c_sb = kb.sb("c_sb", [128, NT]); cT = kb.sb("cT", [128, 128]); lc = kb.sb("lc", [128, 128])
                t32 = kb.sb("t32", [128, 128]); r1 = kb.sb("r1", [128, 128])
                pcs = kb.sb("pcs", [128, 3, 128], BF16)

                def split3(src, R, negate, dst_dram_ap, dst_buf):
                    cur = src
                    for pc in range(3):
                        kb.op("dve", lambda e: e.tensor_copy(out=pcs[0:R, pc, :], in_=cur[0:R, :]), [cur, pcs], [pcs])
                        if pc < 2:
                            kb.op("dve", lambda e: e.tensor_copy(out=t32[0:R, :], in_=pcs[0:R, pc, :]), [pcs], [t32])
                            kb.op("dve", lambda e: e.tensor_tensor(out=r1[0:R, :], in0=cur[0:R, :], in1=t32[0:R, :], op=ALU.subtract), [cur, t32], [r1])
                            cur = r1
                    if negate:
                        kb.op("dve", lambda e: e.tensor_scalar(out=pcs[0:R, :, :], in0=pcs[0:R, :, :], scalar1=-1.0, scalar2=None, op0=ALU.mult), [pcs], [pcs])
                    kb.dma("sp", dst_dram_ap, pcs[0:R, :, :], reads=[pcs], writes=[dst_buf])

                for h in range(4):
                    L = lf[:, h, :]
                    kb.op("pe", lambda e: e.matmul(pimp[:, 0:NT], lhsT=tri[:], rhs=L, start=True, stop=False), [tri, lf], [pimp])
                    kb.op("pe", lambda e: e.matmul(pfin[0:NT, 0:128], lhsT=L, rhs=onesf[:], start=True, stop=True), [lf, onesf], [pfin])
                    kb.op("dve", lambda e: e.tensor_copy(out=totT[0:NT, :], in_=pfin[0:NT, 0:128]), [pfin], [totT])
                    kb.op("pe", lambda e: e.matmul(pimp[:, 0:NT], lhsT=totT[0:NT, :], rhs=strict[0:NT, 0:NT], start=False, stop=True), [totT, strict], [pimp])
                    kb.op("dve", lambda e: e.tensor_copy(out=c_sb[:], in_=pimp[:, 0:NT]), [pimp], [c_sb])
                    kb.op("pe", lambda e: e.transpose(pfin[0:NT, 0:128], c_sb[:, 0:NT], identf[:]), [c_sb, identf], [pfin])
                    kb.op("dve", lambda e: e.tensor_copy(out=cT[0:NT, :], in_=pfin[0:NT, 0:128]), [pfin], [cT])
                    split3(cT, NT, True, csn_d.t[h].rearrange("c (j p) -> j c p", p=128), csn_d)
                    kb.op("pe", lambda e: e.matmul(pimp[0:G, 0:128], lhsT=selm[0:NT, :], rhs=cT[0:NT, :], start=True, stop=True), [selm, cT], [pimp])
                    kb.op("dve", lambda e: e.tensor_copy(out=lc[0:G, :], in_=pimp[0:G, 0:128]), [pimp], [lc])
                    split3(lc, G, False, csq_d.t[h].rearrange("c (i p) -> i c p", p=128), csq_d)
                kb.op("pool", lambda e: e.memset(Kaug[64:72, :], 1.0), [], [Kaug])
                kb.op("pool", lambda e: e.memset(Qa[64:72, :], 1.0), [], [Qa])
                for h in range(4):
                    load_K(512 + 64 * h, (csn_d.t[h], csn_d), (67, 70))
                    load_V(384 + 64 * h)
                    kb.dma("sp", Qa[0:64, :], qt_d[512 + 64 * h:512 + 64 * (h + 1), :], reads=[qt_d], writes=[Qa])
                    kb.dma("sp", Qa[64:67, :], csq_d.t[h], reads=[csq_d], writes=[Qa])
                    ohs = {}

                    def consume_of(i0, h=h):
                        oh = ohead.next()
                        ohs[i0] = oh
                        return plain_consume(i0, oh)
                    for ck in range(NCHUNK):
                        pass
                    def consume_store(i0, h=h):
                        oh = ohead.next()
                        inner = plain_consume(i0, oh)
                        cnt = [0]

                        def cons(u, num, rden):
                            inner(u, num, rden)
                            cnt[0] += 1
                            if cnt[0] == CH:
                                headnorm_store(oh[:, 0:CH, :], oh, CH, i0, 8 + h)
                        return cons
                    causal_head(70, lambda c0, n: Qa[0:70, c0:c0 + n], [Qa], None, consume_store)

        if "sb" in parts:
            with kb.scope():
                sm = load("sm_s", sm_d, [128, 8, 128])
                trineg = load("trineg_s", trineg_d, [128, 128], BF16)
                onesneg = load("onesneg_s", onesneg_d, [128, 128], BF16)
                e1 = Rot([kb.sb("e1_%d" % i, [128, 512]) for i in range(2)])
                spf = Rot([kb.sb("spf%d" % i, [128, 512]) for i in range(3)])
                spb = Rot([kb.sb("spb%d" % i, [128, 512], BF16) for i in range(4)])
                ssum = kb.sb("ssum", [128, 512])
                ssumb = Rot([kb.sb("ssumb%d" % i, [128, 512], BF16) for i in range(5)])
                at = Rot([kb.sb("at%d" % i, [128, 512], BF16) for i in range(4)])
                for h in range(4):
                    load_K(768 + 64 * h)
                    load_V(640 + 64 * h)
                    kb.dma("sp", Qa[0:64, :], qt_d[768 + 64 * h:768 + 64 * (h + 1), :], reads=[qt_d], writes=[Qa])
                    stages = []
                    holder = {}
                    for ck in range(NCHUNK):
                        i0 = ck * CH
                        accb = accr.next()
                        oh = ohead.next()
                        jlist = list(range(8 * (i0 + CH - 1) + 7, -1, -1))
                        for j in jlist:
                            imin = max(i0, j // 8)
                            c0 = (imin - i0) * 128
                            bnd = (j // 8 >= i0)
                            r = j % 8
                            st_ = {}
                            is_first = (j == jlist[0])
                            is_last = (j == jlist[-1])

                            def sA(j=j, c0=c0, bnd=bnd, r=r, st_=st_, is_first=is_first, i0=i0):
                                if is_first:
                                    kb.op("pool", lambda e: e.memset(ssum[:], 0.0), [], [ssum])
                                    z0 = ssumb.next()
                                    kb.op("pool", lambda e: e.memset(z0[:], 0.0), [], [z0])
                                    st_["prev"] = z0
                                else:
                                    st_["prev"] = holder["next"]
                                zb = sps.next()
                                st_["zb"] = zb
                                kb.op("pe", lambda e: e.matmul(zb[:, c0:CT], lhsT=Kaug[0:64, j * 128:(j + 1) * 128], rhs=Qa[0:64, i0 * 128 + c0:i0 * 128 + CT],
                                                               start=True, stop=False, skip_group_check=True), [Kaug, Qa], [zb])
                                ee = e1.next()
                                kb.op("act", lambda e: e.activation(out=ee[:, c0:CT], in_=zb[:, c0:CT], func=AF.Exp), [zb], [ee])
                                sf = spf.next()
                                kb.op("act", lambda e: e.activation(out=sf[:, c0:CT], in_=ee[:, c0:CT], func=AF.Ln, bias=oneb[:]), [ee, oneb], [sf])
                                if bnd:
                                    kb.op("dve", lambda e: e.tensor_tensor(out=sf[:, c0:c0 + 128], in0=sf[:, c0:c0 + 128], in1=sm[:, r, :], op=ALU.mult), [sf, sm], [sf])
                                sbf = spb.next()
                                st_["sbf"] = sbf
                                kb.op("pool", lambda e: e.tensor_copy(out=sbf[:, c0:CT], in_=sf[:, c0:CT]), [sf], [sbf])
                                kb.op("dve", lambda e: e.tensor_tensor(out=ssum[:, c0:CT], in0=ssum[:, c0:CT], in1=sf[:, c0:CT], op=ALU.add), [ssum, sf], [ssum])
                                sb_next = ssumb.next()
                                kb.op("pool", lambda e: e.tensor_copy(out=sb_next[:], in_=ssum[:]), [ssum], [sb_next])
                                holder["next"] = sb_next

                            def sB(c0=c0, bnd=bnd, r=r, st_=st_):
                                zb = st_["zb"]; sbf = st_["sbf"]; sb_prev = st_["prev"]
                                kb.op("pe", lambda e: e.matmul(zb[:, c0:CT], lhsT=trineg[:], rhs=sbf[:, c0:CT], start=False, stop=False, skip_group_check=True),
                                      [trineg, sbf], [zb])
                                kb.op("pe", lambda e: e.matmul(zb[:, c0:CT], lhsT=onesneg[:], rhs=sb_prev[:, c0:CT], start=False, stop=True, skip_group_check=True),
                                      [onesneg, sb_prev], [zb])
                                a_ = at.next()
                                st_["a"] = a_
                                kb.op("act", lambda e: e.activation(out=a_[:, c0:CT], in_=zb[:, c0:CT], func=AF.Exp), [zb], [a_])
                                if bnd:
                                    kb.op("dve", lambda e: e.tensor_tensor(out=a_[:, c0:c0 + 128], in0=a_[:, c0:c0 + 128], in1=sm[:, r, :], op=ALU.mult), [a_, sm], [a_])

                            def sC(j=j, c0=c0, st_=st_, is_first=is_first, is_last=is_last, accb=accb, oh=oh, i0=i0, h=h):
                                a_ = st_["a"]
                                kb.op("pe", lambda e: e.matmul(accb[0:65, c0:CT], lhsT=Vaug[:, j, :], rhs=a_[:, c0:CT], start=is_first, stop=False, skip_group_check=True),
                                      [Vaug, a_], [accb])
                                if is_last:
                                    def cons_sb(u, num, rden):
                                        kb.op("dve", lambda e: e.tensor_copy(out=oh[:, u, :], in_=num), [pfin], [oh])
                                    finalize(accb, CH, cons_sb)
                                    headnorm_store(oh[:, 0:CH, :], oh, CH, i0, 12 + h)
                            stages.append((sA, sB, sC))
                    n_ = len(stages)
                    for t in range(n_ + 2):
                        if t < n_:
                            stages[t][0]()
                        if 0 <= t - 1 < n_:
                            stages[t - 1][1]()
                        if 0 <= t - 2 < n_:
                            stages[t - 2][2]()
        if kb_in is None:
            kb.finish([o_d])
            print("attn program: ninst=%d nwait=%d nsem=%d" % (kb.ninst, kb.nwait, kb.nsem))
    return nc


def build_merged(G, do_pre, do_final):
    TL = G * 128
    nc = bass.Bass("TRN2", target_bir_lowering=False)
    o_int = Buf("o_int", nc.dram_tensor("o_int", [TL, D], F32, kind="Internal").ap())
    identb_d = Buf("identb", nc.dram_tensor("identb", [128, 128], BF16, kind="ExternalInput").ap())
    with contextlib.ExitStack() as st:
        kb = KB(nc, st)
        build_attn(G, nc=nc, kb_in=kb, given={"o": o_int, "identb": identb_d})
        _, outs = build_rowlocal(G, True, do_pre, do_final, nc=nc, kb_in=kb, given={"o": o_int, "identb": identb_d})
        kb.finish(outs)
        print("merged program: ninst=%d nwait=%d nsem=%d" % (kb.ninst, kb.nwait, kb.nsem))
    return nc


_PROG_CACHE = {}


def _get_prog(key, fn):
    if key not in _PROG_CACHE:
        _PROG_CACHE[key] = fn()
    return _PROG_CACHE[key]


def _gcol(g):
    return np.ascontiguousarray(np.asarray(g, np.float32).reshape(8, 128).T)


def _local_rows(G, c):
    return np.concatenate([np.arange((8 * i + c) * 128, (8 * i + c + 1) * 128) for i in range(G)])


def kernel(x, p, g_mix, w_in, b_nsa_gate, b_forget, cmp_pos_k, cmp_w1_k, cmp_w2_k,
           cmp_pos_v, cmp_w1_v, cmp_w2_v, g_head, w_out, g_mlp, w_up, w_down,
           g_ple, w_ple_gate, b_ple_gate, w_ple_proj, g_final):
    f32 = lambda a: np.asarray(a, dtype=np.float32)
    x = f32(x); p = f32(p)
    S = x.shape[1]
    G = S // 1024
    NT = 8 * G
    depth = w_in.shape[0]
    cores = list(range(NCORES))
    rows = [_local_rows(G, c) for c in cores]
    identb = np.eye(128).astype(NPBF)
    consts = [attn_consts(G, c) for c in cores]
    h = [np.ascontiguousarray(x[0][r]) for r in rows]

    def pre_inputs(l):
        return {"w_in": np.ascontiguousarray(f32(w_in[l])[:, PERM]), "g_mix": _gcol(g_mix[l]),
                "b_gf": np.concatenate([f32(b_nsa_gate[l]), f32(b_forget[l])])[None]}

    def post_inputs(l, c):
        return {"p": np.ascontiguousarray(p[l, 0][rows[c]]), "w_out": f32(w_out[l]),
                "gcols": np.ascontiguousarray(np.concatenate([_gcol(g_head[l]), _gcol(g_mlp[l]), _gcol(g_ple[l])], axis=1)),
                "w_up": f32(w_up[l]), "w_down": f32(w_down[l]), "w_pg": f32(w_ple_gate[l]),
                "b_pg": f32(b_ple_gate[l])[None], "w_pp": f32(w_ple_proj[l])}

    def attn_inputs(l, res, c, kt_g, v_g, logf_g):
        m = {"qt": np.asarray(res[c]["qt_o"]), "kt": kt_g, "v": v_g, "gfl": np.asarray(res[c]["gf_o"]), "logf": logf_g,
             "w1": np.stack([f32(cmp_w1_k[l]), f32(cmp_w1_v[l])]), "w2": np.stack([f32(cmp_w2_k[l]), f32(cmp_w2_v[l])]),
             "posT": np.ascontiguousarray(np.concatenate([f32(cmp_pos_k[l]).T, f32(cmp_pos_v[l]).T], axis=0))}
        m.update(consts[c])
        return m

    def exchange(res):
        kt_g = np.zeros((1024, S), NPBF)
        v_g = np.zeros((S, 896), NPBF)
        lf = np.zeros((S, 4), np.float32)
        for c in cores:
            kt_g[:, rows[c]] = np.asarray(res[c]["kt_o"])
            v_g[rows[c]] = np.asarray(res[c]["v_o"])
            lf[rows[c]] = np.asarray(res[c]["gf_o"])[:, 12:16]
        return kt_g, v_g, np.ascontiguousarray(lf.reshape(NT, 128, 4).transpose(1, 2, 0))

    nc0 = _get_prog(("pre", G), lambda: build_rowlocal(G, False, True, False)[0])
    in_maps = []
    for c in cores:
        m = {"h": h[c], "identb": identb}
        m.update(pre_inputs(0))
        in_maps.append(m)
    res = run_bass_kernel_spmd(nc0, in_maps, core_ids=cores).results
    out = None
    for l in range(depth):
        last = (l == depth - 1)
        kt_g, v_g, logf_g = exchange(res)
        ncm = _get_prog(("merged", G, last), lambda: build_merged(G, not last, last))
        in_maps = []
        for c in cores:
            m = attn_inputs(l, res, c, kt_g, v_g, logf_g)
            m.update(post_inputs(l, c))
            m["h"] = h[c]
            if last:
                m["g_fin"] = f32(g_final)[None]
            else:
                m.update(pre_inputs(l + 1))
            in_maps.append(m)
        res = run_bass_kernel_spmd(ncm, in_maps, core_ids=cores).results
        if last:
            out = np.zeros((1, S, D), np.float32)
            for c in cores:
                out[0][rows[c]] = res[c]["out"]
        else:
            h = [np.asarray(res[c]["h_out"]) for c in cores]
    return out
```
